# Optimizing a Trainium2 kernel written in Bass

```python
import math
import jax, jax.numpy as jnp
from jax import lax
import numpy as np

D_MODEL = 1024
BATCH = 8
SEQ = 4096
DEPTH = 1

HEAD_DIM = 64
SWA_Q_HEADS = 8
SWA_KV_HEADS = 2
SWA_GROUP = SWA_Q_HEADS // SWA_KV_HEADS
WINDOW = 128
SWA_WIDTH = SWA_Q_HEADS * HEAD_DIM
DIFF_HEADS = 4
DIFF_V_DIM = 2 * HEAD_DIM
DIFF_WIDTH = DIFF_HEADS * DIFF_V_DIM
MIX_WIDTH = SWA_WIDTH + DIFF_WIDTH
Q_BLOCK = 128
RMS_EPS = 1e-6
SUBLN_EPS = 1e-5

A_Q_COLS = SWA_Q_HEADS * HEAD_DIM
A_K_COLS = SWA_KV_HEADS * HEAD_DIM
A_V_COLS = SWA_KV_HEADS * HEAD_DIM
B_Q_COLS = DIFF_HEADS * 2 * HEAD_DIM
B_K_COLS = DIFF_HEADS * 2 * HEAD_DIM
B_V_COLS = DIFF_HEADS * DIFF_V_DIM
GATE_COLS = MIX_WIDTH
IN_COLS = A_Q_COLS + A_K_COLS + A_V_COLS + B_Q_COLS + B_K_COLS + B_V_COLS + GATE_COLS

kernel_name = "hybrid_swa_sink_diffattn_gated_alibi"


def rmsnorm(x, g, eps=RMS_EPS):
    xf = x.astype(jnp.float32)
    y = xf * lax.rsqrt(jnp.mean(xf * xf, axis=-1, keepdims=True) + eps)
    return (y * g.astype(jnp.float32)).astype(x.dtype)


def alibi_slopes(n_heads):
    return 2.0 ** (-8.0 * jnp.arange(1, n_heads + 1, dtype=jnp.float32) / n_heads)


def lambda_init_fn(layer_idx):
    return 0.8 - 0.6 * math.exp(-0.3 * layer_idx)


def sliding_window_gqa_sinks(q, k, v, sinks):
    B, S = q.shape[0], q.shape[1]
    nb = S // WINDOW
    qb = q.reshape(B, nb, WINDOW, SWA_KV_HEADS, SWA_GROUP, HEAD_DIM)
    kb = k.reshape(B, nb, WINDOW, SWA_KV_HEADS, HEAD_DIM)
    vb = v.reshape(B, nb, WINDOW, SWA_KV_HEADS, HEAD_DIM)
    pad = ((0, 0), (1, 0), (0, 0), (0, 0), (0, 0))
    k_band = jnp.concatenate([jnp.pad(kb, pad)[:, :-1], kb], axis=2)
    v_band = jnp.concatenate([jnp.pad(vb, pad)[:, :-1], vb], axis=2)
    scale = HEAD_DIM ** -0.5
    s = jnp.einsum('bnqhgd,bnkhd->bhgnqk', qb, k_band).astype(jnp.float32) * scale
    q_pos = jnp.arange(WINDOW)[:, None] + WINDOW
    k_pos = jnp.arange(2 * WINDOW)[None, :]
    dist = q_pos - k_pos
    valid = (dist >= 0) & (dist < WINDOW)
    not_first = (jnp.arange(nb)[:, None, None] > 0) | (k_pos >= WINDOW)[None]
    valid = valid[None] & not_first
    slopes = alibi_slopes(SWA_Q_HEADS).reshape(SWA_KV_HEADS, SWA_GROUP)
    s = s - slopes[:, :, None, None, None] * dist.astype(jnp.float32)
    s = jnp.where(valid, s, -jnp.inf)
    sink = sinks.astype(jnp.float32).reshape(SWA_KV_HEADS, SWA_GROUP)[:, :, None, None, None]
    m = jnp.maximum(jnp.max(s, axis=-1, keepdims=True), sink)
    p = jnp.exp(s - m)
    probs = p / (jnp.sum(p, axis=-1, keepdims=True) + jnp.exp(sink - m))
    out = jnp.einsum('bhgnqk,bnkhd->bnqhgd', probs.astype(v.dtype), v_band)
    return out.reshape(B, S, SWA_WIDTH)


def differential_attention(q, k, v, lam, subln_g, lambda_init):
    B, S = q.shape[0], q.shape[1]
    nb = S // Q_BLOCK
    qb = q.reshape(B, nb, Q_BLOCK, DIFF_HEADS, 2, HEAD_DIM).transpose(1, 0, 2, 3, 4, 5)
    slopes = alibi_slopes(DIFF_HEADS)
    scale = HEAD_DIM ** -0.5
    k_pos = jnp.arange(S)

    def one_block(args):
        blk, q_blk = args
        q_pos = blk * Q_BLOCK + jnp.arange(Q_BLOCK)
        dist = q_pos[:, None] - k_pos[None, :]
        s = jnp.einsum('bqhcd,bkhcd->bhcqk', q_blk, k).astype(jnp.float32) * scale
        s = s - slopes[:, None, None, None] * dist.astype(jnp.float32)
        s = jnp.where(dist >= 0, s, -jnp.inf)
        p = jax.nn.softmax(s, axis=-1)
        a = p[:, :, 0] - lam * p[:, :, 1]
        return jnp.einsum('bhqk,bkhe->bqhe', a.astype(v.dtype), v)

    out = lax.map(one_block, (jnp.arange(nb), qb))
    out = out.transpose(1, 0, 2, 3, 4).reshape(B, S, DIFF_HEADS, DIFF_V_DIM)
    out = rmsnorm(out, subln_g, SUBLN_EPS) * (1.0 - lambda_init)
    return out.reshape(B, S, DIFF_WIDTH)


def hybrid_layer(x, norm_g, w_in, sinks, lq1, lk1, lq2, lk2, subln_g, w_out, layer_idx):
    B, S, _ = x.shape
    h = rmsnorm(x, norm_g)
    proj = jnp.einsum('bsd,dc->bsc', h, w_in)
    cuts = np.cumsum([A_Q_COLS, A_K_COLS, A_V_COLS, B_Q_COLS, B_K_COLS, B_V_COLS])
    aq, ak, av, bq, bk, bv, gate = jnp.split(proj, cuts, axis=-1)
    aq = aq.reshape(B, S, SWA_Q_HEADS, HEAD_DIM)
    ak = ak.reshape(B, S, SWA_KV_HEADS, HEAD_DIM)
    av = av.reshape(B, S, SWA_KV_HEADS, HEAD_DIM)
    out_a = sliding_window_gqa_sinks(aq, ak, av, sinks)
    lambda_init = lambda_init_fn(layer_idx)
    lam = (jnp.exp(jnp.sum(lq1.astype(jnp.float32) * lk1.astype(jnp.float32)))
           - jnp.exp(jnp.sum(lq2.astype(jnp.float32) * lk2.astype(jnp.float32)))
           + lambda_init)
    bq = bq.reshape(B, S, DIFF_HEADS, 2, HEAD_DIM)
    bk = bk.reshape(B, S, DIFF_HEADS, 2, HEAD_DIM)
    bv = bv.reshape(B, S, DIFF_HEADS, DIFF_V_DIM)
    out_b = differential_attention(bq, bk, bv, lam, subln_g, lambda_init)
    mixed = jnp.concatenate([out_a, out_b], axis=-1) * jax.nn.silu(gate)
    return x + jnp.einsum('bsc,cd->bsd', mixed, w_out)


def setup_inputs(seed: int = 0) -> dict:
    key = jax.random.key(seed)
    ks = jax.random.split(key, 12)
    f32 = jnp.float32
    x = jax.random.normal(ks[0], (BATCH, SEQ, D_MODEL), f32)
    norm_g = 1.0 + 0.02 * jax.random.normal(ks[1], (DEPTH, D_MODEL), f32)
    w_in = jax.random.normal(ks[2], (DEPTH, D_MODEL, IN_COLS), f32) * D_MODEL ** -0.5
    sinks = 0.5 * jax.random.normal(ks[3], (DEPTH, SWA_Q_HEADS), f32)
    lq1 = 0.1 * jax.random.normal(ks[4], (DEPTH, HEAD_DIM), f32)
    lk1 = 0.1 * jax.random.normal(ks[5], (DEPTH, HEAD_DIM), f32)
    lq2 = 0.1 * jax.random.normal(ks[6], (DEPTH, HEAD_DIM), f32)
    lk2 = 0.1 * jax.random.normal(ks[7], (DEPTH, HEAD_DIM), f32)
    subln_g = 1.0 + 0.02 * jax.random.normal(ks[8], (DEPTH, DIFF_V_DIM), f32)
    w_out = jax.random.normal(ks[9], (DEPTH, MIX_WIDTH, D_MODEL), f32) * MIX_WIDTH ** -0.5
    final_g = 1.0 + 0.02 * jax.random.normal(ks[10], (D_MODEL,), f32)
    return {"x": x, "norm_g": norm_g, "w_in": w_in, "sinks": sinks,
            "lambda_q1": lq1, "lambda_k1": lk1, "lambda_q2": lq2, "lambda_k2": lk2,
            "subln_g": subln_g, "w_out": w_out, "final_g": final_g}


def reference(x, norm_g, w_in, sinks, lambda_q1, lambda_k1, lambda_q2, lambda_k2,
              subln_g, w_out, final_g):
    h = x
    for layer in range(DEPTH):
        h = hybrid_layer(h, norm_g[layer], w_in[layer], sinks[layer],
                         lambda_q1[layer], lambda_k1[layer], lambda_q2[layer], lambda_k2[layer],
                         subln_g[layer], w_out[layer], layer)
    return rmsnorm(h, final_g)
```

```python
import numpy as np
import concourse.bass as bass
import concourse.mybir as mybir
from concourse.bass_utils import run_bass_kernel_spmd

F32 = mybir.dt.float32
BF16 = mybir.dt.bfloat16
AF = mybir.ActivationFunctionType
ALU = mybir.AluOpType

D_MODEL = 1024
SEQ = 4096
NCORES = 8
CH = 256
NCH = SEQ // CH
NDC = D_MODEL // 128
NFB = 21
VOFF = NFB * 128
NCOLS = 3328
NCBF = 3200 + 512
MASKV = -30000.0
LAMBDA_INIT = 0.2
RMS_EPS = 1e-6
SUBLN_EPS = 1e-5


def _win_perm():
    cols = []
    for b in range(4):
        cols += list(range(64 * b, 64 * b + 64)) + list(range(64 * (4 + b), 64 * (4 + b) + 64))
    cols += list(range(512, 640))
    for h in range(4):
        cols += list(range(768 + 128 * h, 768 + 128 * h + 128))
    for h in range(4):
        cols += list(range(1280 + 128 * h, 1280 + 128 * h + 128))
    for b in range(4):
        cols += list(range(2304 + 64 * b, 2304 + 64 * b + 64))
        cols += list(range(2304 + 64 * (4 + b), 2304 + 64 * (4 + b) + 64))
    for h in range(4):
        cols += list(range(2304 + 512 + 128 * h, 2304 + 512 + 128 * h + 128))
    cols += list(range(640, 768))
    cols += list(range(1792, 2304))
    assert len(cols) == NCOLS and len(set(cols)) == NCOLS
    return np.array(cols)


def _wout_perm():
    rows = []
    for b in range(4):
        rows += list(range(64 * b, 64 * b + 64)) + list(range(64 * (4 + b), 64 * (4 + b) + 64))
    rows += list(range(512, 1024))
    return np.array(rows)


def _const_tables():
    k = np.arange(128)[:, None]
    cbf = np.zeros((128, NCBF), np.float32)
    cbf[:, 0:128] = np.eye(128, dtype=np.float32)
    j = np.arange(256)[None, :]
    for t in range(2):
        m = np.where(j >= 128 * t + k, 0.0, MASKV).astype(np.float32)
        cbf[:, 128 + 512 * t: 128 + 512 * t + 256] = m
        cbf[:, 128 + 512 * t + 256: 128 + 512 * t + 512] = m
    q = np.arange(128)[None, :]
    for b in range(4):
        tile = np.zeros((128, 2, 2, 128), np.float32)
        for m in range(2):
            head = b if m == 0 else 4 + b
            slope = 2.0 ** (-(head + 1))
            d0 = q + 128 - k
            tile[:, 0, m, :] = np.where(k > q, -8.0 * slope * d0, MASKV)
            d1 = q - k
            tile[:, 1, m, :] = np.where(k <= q, -8.0 * slope * d1, MASKV)
        cbf[:, 128 + 1024 + 512 * b: 128 + 1024 + 512 * (b + 1)] = tile.reshape(128, 512)
    import ml_dtypes
    wr = np.zeros((128, 4), np.float32)
    for h in range(4):
        slope = 2.0 ** (-2.0 * (h + 1))
        wr[:, h] = np.exp(slope * np.arange(128)).astype(ml_dtypes.bfloat16).astype(np.float32)
        cbf[:, 3200 + 128 * h: 3200 + 128 * (h + 1)] = wr[:, h:h + 1]
    return cbf, wr


class _Op:
    __slots__ = ("eng", "emit", "deps", "is_dma", "group", "cum", "signal", "seq")

    def __init__(self, eng, emit, is_dma=False, group=None):
        self.eng = eng
        self.emit = emit
        self.deps = []
        self.is_dma = is_dma
        self.group = group
        self.cum = 0
        self.signal = False
        self.seq = 0


class _Tracker:
    ENGS = ("pe", "act", "dve", "pool", "sp")

    def __init__(self):
        self.ops = []
        self.last_w = {}
        self.readers = {}
        self.group_cnt = {}

    def add(self, eng, emit, reads=(), writes=(), dma=None):
        op = _Op(eng, emit, is_dma=dma is not None, group=dma)
        if dma is not None:
            self.group_cnt[dma] = self.group_cnt.get(dma, 0) + 16
            op.cum = self.group_cnt[dma]
        deps = set()
        for r in reads:
            w = self.last_w.get(r)
            if w is not None:
                deps.add(w)
        for w_ in writes:
            w = self.last_w.get(w_)
            if w is not None:
                deps.add(w)
            deps |= self.readers.get(w_, set())
        for d in deps:
            if d is op:
                continue
            if (not d.is_dma) and (not op.is_dma) and d.eng == "pe" and eng == "pe":
                continue
            if d.is_dma and op.is_dma and d.group == op.group:
                continue
            op.deps.append(d)
            if not d.is_dma:
                d.signal = True
        for r in reads:
            self.readers.setdefault(r, set()).add(op)
        for w_ in writes:
            self.last_w[w_] = op
            self.readers[w_] = set()
        self.ops.append(op)
        return op


def build_nc():
    nc = bass.Bass("TRN2", target_bir_lowering=False)
    xT = nc.dram_tensor("xT", [D_MODEL, SEQ], F32, kind="ExternalInput").ap()
    w_in = nc.dram_tensor("w_in", [D_MODEL, NCOLS], F32, kind="ExternalInput").ap()
    w_out = nc.dram_tensor("w_out", [D_MODEL, D_MODEL], F32, kind="ExternalInput").ap()
    gcols = nc.dram_tensor("gcols", [128, 21], F32, kind="ExternalInput").ap()
    sinks = nc.dram_tensor("sinks", [1, 8], F32, kind="ExternalInput").ap()
    lamv = nc.dram_tensor("lamv", [1, 256], F32, kind="ExternalInput").ap()
    cbf_d = nc.dram_tensor("cbf", [128, NCBF], F32, kind="ExternalInput").ap()
    outT = nc.dram_tensor("outT", [D_MODEL, SEQ], F32, kind="ExternalOutput").ap()

    xT_v = xT.rearrange("(dc p) t -> p dc t", p=128)
    outT_v = outT.rearrange("(dc p) t -> p dc t", p=128)

    from contextlib import ExitStack
    es = ExitStack()

    def sb(name, shape, dt):
        return es.enter_context(nc.sbuf_tensor(name, shape, dt))

    wbf = sb("wbf", [128, NDC, NCOLS], BF16)
    woutbf = sb("woutbf", [128, 8, D_MODEL], BF16)
    KTa = sb("KTa", [128, SEQ], BF16)
    KTb = sb("KTb", [128, 4, SEQ], BF16)
    Va = sb("Va", [128, 32, 128], BF16)
    Vb = sb("Vb", [128, 32, 512], BF16)
    cbf = sb("cbfs", [128, NCBF], BF16)
    gc = sb("gcs", [128, 21], F32)
    esink = sb("esink", [128, 8], F32)
    cexp = sb("cexp", [128, 2], F32)
    lam2 = sb("lam2", [128, 2], F32)
    neglam = sb("neglam", [128, 1], F32)
    sgc = sb("sgc", [128, 1], F32)
    ones = sb("ones", [128, 128], BF16)
    xs = [sb(f"xs{i}", [128, NDC, CH], F32) for i in range(2)]
    sq = sb("sqh", [128, NDC, CH], BF16)
    hT = sq
    QTa = sb("QTa", [128, 4, 2, CH], BF16)
    QTb = sb("QTb", [128, 4, 2, CH], BF16)
    gs = sb("gs", [128, 8, CH], BF16)
    mixedT = sb("mixedT", [128, 8, CH], BF16)
    Pt_ = [sb(f"Pt{i}", [128, 512], BF16) for i in range(3)]
    rstdA = sb("rstdA", [128, CH], F32)
    rstdT = sb("rstdT", [128, CH], F32)
    gt = [sb("gt0", [128, 2 * CH], F32)]
    T1 = [sb(f"T1_{i}", [128, 512], F32) for i in range(1)]
    oc = sb("oc", [128, 512], F32)
    t1s = sb("t1s", [128, 512], F32)
    sqt = T1[0][:].bitcast(BF16).rearrange("p (a b) -> p a b", a=4)
    lamt = oc[:, 0:256]
    lamp = oc[:, 256:384]
    Dd = [sb(f"Dd{i}", [128, CH], F32) for i in range(1)]
    Dsq = [sb(f"Dsq{i}", [128, CH], BF16) for i in range(1)]
    T3 = [sb(f"T3_{i}", [128, CH], F32) for i in range(1)]

    banks = [es.enter_context(nc.psum_tensor(f"bank{i}", [128, 512], F32)) for i in range(8)]
    NS = 3
    S_b = banks[0:3]
    O_b = banks[3:5]
    Z_b = banks[5:7]
    G_b = banks[7:8]

    ident = cbf[:, 0:128]

    def stair(t):
        return cbf[:, 128 + 512 * t: 128 + 512 * (t + 1)]

    def wtile(h):
        return cbf[:, 3200 + 128 * h: 3200 + 128 * (h + 1)]

    def band(b):
        return cbf[:, 128 + 1024 + 512 * b: 128 + 1024 + 512 * (b + 1)].rearrange(
            "p (s m q) -> p s m q", s=2, m=2)

    tr = _Tracker()
    A = tr.add
    cnt = {"s": 0, "oz": 0, "p": 0, "gt": 0, "sqt": 0}

    def rot(key, n):
        v = cnt[key] % n
        cnt[key] += 1
        return v

    class Chain:
        def __init__(self, gen, after=(), rate=1):
            self.gen = gen
            self.after = [a for a in after if a is not None]
            self.done = False
            self.rate = rate

    chains = []
    gfree = [True]

    def start_chain(gen, after=(), rate=1):
        ch = Chain(gen, after, rate)
        chains.append(ch)
        return ch

    def _step(ch):
        if ch.done:
            return
        if any(not a.done for a in ch.after):
            return
        try:
            next(ch.gen)
        except StopIteration:
            ch.done = True

    def pump():
        for ch in list(chains):
            for _ in range(ch.rate):
                _step(ch)
        chains[:] = [ch for ch in chains if not ch.done]

    def wait_chain(ch):
        guard = 0
        while ch is not None and not ch.done:
            pump()
            guard += 1
            assert guard < 10000

    def flush():
        guard = 0
        while chains:
            pump()
            guard += 1
            assert guard < 10000

    def acquire_g():
        while True:
            for i in range(len(gfree)):
                if gfree[i]:
                    gfree[i] = False
                    return i
            yield

    def load_x(c):
        s = c % 2
        A("sp", lambda e: e.dma_start(out=xs[s][:], in_=xT_v[:, :, c * CH:(c + 1) * CH]),
          writes=[f"xs{s}"], dma=f"ld{s}")

    wgroups = [(0, 640, "wA"), (640, 1664, "wB"), (1664, 2688, "wC"), (2688, 3328, "wD")]

    def wres(lo, hi):
        return [nm for (a, b_, nm) in wgroups if a < hi and lo < b_]

    def load_w(lo, hi, nm):
        for dc in range(NDC):
            A("pool", lambda e, dc=dc: e.dma_start(
                out=wbf[:, dc, lo:hi], in_=w_in[dc * 128:(dc + 1) * 128, lo:hi]),
              writes=[nm], dma=nm)

    load_x(0)
    A("sp", lambda e: e.dma_start(out=gc[:], in_=gcols[:, :]), writes=["gc"], dma="c1")
    A("pool", lambda e: e.memset(ones[:], 1.0), writes=["ones"])
    load_w(*wgroups[0])
    A("pool", lambda e: e.memset(QTa[:].rearrange("p a b c -> p (a b c)"), 0.0),
      writes=[f"QTa{b}_{m}" for b in range(4) for m in range(2)])
    load_w(*wgroups[1])
    A("pool", lambda e: e.memset(QTb[:].rearrange("p a b c -> p (a b c)"), 0.0),
      writes=[f"QTb{b}_{m}" for b in range(4) for m in range(2)])
    load_w(*wgroups[2])
    A("pool", lambda e: e.dma_start(out=cbf[:, 0:1856], in_=cbf_d[:, 0:1856]), writes=["cbf"], dma="c0")
    A("pool", lambda e: e.dma_start(out=cbf[:, 1856:NCBF], in_=cbf_d[:, 1856:NCBF]), writes=["cbf"], dma="c0")
    load_w(*wgroups[3])
    A("sp", lambda e: e.dma_start(out=esink[:], in_=sinks[0:1, :].partition_broadcast(128)),
      writes=["esink"], dma="c2")
    A("sp", lambda e: e.dma_start(out=lamt, in_=lamv[0:1, :].partition_broadcast(128)),
      writes=["oc"], dma="c3")
    A("pool", lambda e: e.memset(cexp[:, 0:1], -1.0), writes=["cexp"])
    A("pool", lambda e: e.memset(cexp[:, 1:2], -0.5), writes=["cexp"])
    load_x(1)
    for j in range(8):
        A("pool", lambda e, j=j: e.dma_start(out=woutbf[:, j, :], in_=w_out[j * 128:(j + 1) * 128, :]),
          writes=["woutbf"], dma="wo")

    A("act", lambda e: e.activation(out=esink[:], in_=esink[:], func=AF.Exp), reads=["esink"], writes=["esink"])
    A("dve", lambda e: e.tensor_tensor(out=lamp.rearrange("p (a b) -> p a b", a=2),
                                       in0=lamt.rearrange("p (a b c) -> p a b c", a=2, b=2)[:, :, 0, :],
                                       in1=lamt.rearrange("p (a b c) -> p a b c", a=2, b=2)[:, :, 1, :],
                                       op=ALU.mult), reads=["oc"], writes=["oc"])
    A("dve", lambda e: e.reduce_sum(out=lam2[:], in_=lamp.rearrange("p (a b) -> p a b", a=2),
                                    axis=mybir.AxisListType.X), reads=["oc"], writes=["lam2"])
    A("act", lambda e: e.activation(out=lam2[:], in_=lam2[:], func=AF.Exp), reads=["lam2"], writes=["lam2"])
    A("dve", lambda e: e.scalar_tensor_tensor(out=neglam[:], in0=lam2[:, 1:2], scalar=-LAMBDA_INIT,
                                              in1=lam2[:, 0:1], op0=ALU.add, op1=ALU.subtract),
      reads=["lam2"], writes=["neglam"])
    A("dve", lambda e: e.tensor_scalar(out=sgc[:], in0=gc[:, 16:17], scalar1=1.0 - LAMBDA_INIT, scalar2=None,
                                       op0=ALU.mult), reads=["gc"], writes=["sgc"])

    def evac(out_ap, in_ap, reads, writes, eng="dve"):
        if eng == "act":
            A("act", lambda e: e.activation(out=out_ap, in_=in_ap, func=AF.Copy), reads=reads, writes=writes)
        else:
            A("dve", lambda e: e.tensor_copy(out=out_ap, in_=in_ap), reads=reads, writes=writes)

    def pow_bc(col, n):
        return cexp[:, col:col + 1].to_broadcast([128, n])

    def chain_A(c):
        s = c % 2
        xsn, xsc = f"xs{s}", xs[s]
        if c > 0:
            for _ in range(10):
                yield
        for half in range(2):
            A("dve", lambda e, half=half: e.tensor_tensor(
                out=sq[:, 4 * half:4 * half + 4, :].rearrange("p a b -> p (a b)"),
                in0=xsc[:, 4 * half:4 * half + 4, :].rearrange("p a b -> p (a b)"),
                in1=xsc[:, 4 * half:4 * half + 4, :].rearrange("p a b -> p (a b)"), op=ALU.mult),
              reads=[xsn], writes=["sqh"])
            yield
        yield
        yield
        g = yield from acquire_g()
        for dc in range(NDC):
            A("pe", lambda e, dc=dc: e.matmul(G_b[g][:, 0:CH], ones[:], sq[:, dc, :],
                                              start=(dc == 0), stop=(dc == NDC - 1)),
              reads=["sqh", "ones"], writes=[f"G{g}"])
        yield
        A("act", lambda e: e.activation(out=rstdA[:], in_=G_b[g][:, 0:CH], func=AF.Ln,
                                        bias=RMS_EPS, scale=1.0 / D_MODEL), reads=[f"G{g}"], writes=["rstdA"])
        gfree[g] = True
        A("act", lambda e: e.activation(out=rstdA[:], in_=rstdA[:], func=AF.Exp, scale=-0.5),
          reads=["rstdA"], writes=["rstdA"])
        yield
        for q4 in range(4):
            for dc in range(2 * q4, 2 * q4 + 2):
                A("dve", lambda e, dc=dc: e.scalar_tensor_tensor(
                    out=hT[:, dc, :], in0=xsc[:, dc, :], scalar=gc[:, dc:dc + 1], in1=rstdA[:],
                    op0=ALU.mult, op1=ALU.mult), reads=[xsn, "gc", "rstdA"], writes=["sqh"])
            yield

    def chain_E1(c):
        s = c % 2
        xsn, xsc = f"xs{s}", xs[s]
        mxall = [f"mx{j}" for j in range(8)]
        for db in range(NDC):
            g = yield from acquire_g()
            for j in range(8):
                A("pe", lambda e, j=j, db=db, g=g: e.matmul(
                    G_b[g][:, 0:CH], woutbf[:, j, db * 128:(db + 1) * 128], mixedT[:, j, :],
                    start=(j == 0), stop=(j == 7)), reads=["woutbf"] + mxall, writes=[f"G{g}"])
            yield
            A("dve", lambda e, db=db, g=g: e.tensor_tensor(
                out=xsc[:, db, :], in0=G_b[g][:, 0:CH], in1=xsc[:, db, :], op=ALU.add),
              reads=[f"G{g}", xsn], writes=[xsn])
            gfree[g] = True
            yield

    def chain_E2(c):
        s = c % 2
        xsn, xsc = f"xs{s}", xs[s]
        g = None
        for half in range(2):
            for dc in range(4 * half, 4 * half + 4):
                A("act", lambda e, dc=dc: e.activation(out=sqt[:, dc % 4, :], in_=xsc[:, dc, :], func=AF.Square),
                  reads=[xsn], writes=["T1_0"])
            yield
            yield
            if g is None:
                g = yield from acquire_g()
            for dc in range(4 * half, 4 * half + 4):
                A("pe", lambda e, dc=dc, g=g: e.matmul(G_b[g][:, 0:CH], ones[:], sqt[:, dc % 4, :],
                                                       start=(dc == 0), stop=(dc == NDC - 1)),
                  reads=["T1_0", "ones"], writes=[f"G{g}"])
            yield
        A("act", lambda e: e.activation(out=rstdT[:], in_=G_b[g][:, 0:CH], func=AF.Ln,
                                        bias=RMS_EPS, scale=1.0 / D_MODEL), reads=[f"G{g}"], writes=["rstdT"])
        gfree[g] = True
        A("act", lambda e: e.activation(out=rstdT[:], in_=rstdT[:], func=AF.Exp, scale=-0.5),
          reads=["rstdT"], writes=["rstdT"])
        yield
        for q4 in range(4):
            for dc in range(2 * q4, 2 * q4 + 2):
                A("dve", lambda e, dc=dc: e.scalar_tensor_tensor(
                    out=xsc[:, dc, :], in0=xsc[:, dc, :], scalar=gc[:, 8 + dc:9 + dc], in1=rstdT[:],
                    op0=ALU.mult, op1=ALU.mult), reads=[xsn, "gc", "rstdT"], writes=[xsn])
            yield
        A("sp", lambda e: e.dma_start(out=outT_v[:, :, c * CH:(c + 1) * CH], in_=xsc[:]),
          reads=[xsn], writes=[f"out{c}"], dma="st")
        if c + 2 < NCH:
            load_x(c + 2)

    def chain_swa_epi(c, b, z):
        On, Zn = f"O{z}", f"Z{z}"
        Z4 = Z_b[z][:].rearrange("p (i m q) -> p i m q", i=2, m=2)
        t1, t1n = t1s, "t1s"
        t14 = t1[:].rearrange("p (i m q) -> p i m q", i=2, m=2)
        for m in range(2):
            head = b if m == 0 else 4 + b
            A("dve", lambda e, m=m, head=head: e.tensor_scalar(
                out=t14[:, :, m, :], in0=Z4[:, :, m, :], scalar1=esink[:, head:head + 1], scalar2=None,
                op0=ALU.add), reads=[Zn, "esink"], writes=[t1n])
        A("dve", lambda e: e.tensor_copy(out=oc[:], in_=O_b[z][:]), reads=[On], writes=["oc"])
        slot_free[z] = True
        yield
        A("act", lambda e: e.activation(out=t1[:], in_=t1[:], func=AF.Ln), reads=[t1n], writes=[t1n])
        yield
        A("act", lambda e: e.activation(out=t1[:], in_=t1[:], func=AF.Exp, scale=-1.0), reads=[t1n], writes=[t1n])
        yield
        yield
        A("dve", lambda e: e.tensor_tensor(out=t1[:], in0=oc[:], in1=t1[:], op=ALU.mult),
          reads=["oc", t1n], writes=[t1n])
        e1p = st["E1"].get(c - 1)
        while e1p is not None and not e1p.done:
            yield
        for m in range(2):
            pr = slice(0, 64) if m == 0 else slice(64, 128)
            A("dve", lambda e, m=m, pr=pr: e.tensor_tensor(
                out=mixedT[pr, b, :].rearrange("p (i q) -> p i q", i=2), in0=t14[pr, :, m, :],
                in1=gs[pr, b, :].rearrange("p (i q) -> p i q", i=2), op=ALU.mult),
              reads=[t1n, f"gs{b}"], writes=[f"mx{b}"])

    def chain_diff_epi(c, h, z):
        On, Zn = f"O{z}", f"Z{z}"
        t1, dd, dsq, t3 = T1[0], Dd[0], Dsq[0], T3[0]
        t1n, ddn, dsqn, t3n = "T1_0", "Dd0", "Dsq0", "T3_0"
        A("act", lambda e: e.activation(out=t1[:], in_=Z_b[z][:, :], func=AF.Ln), reads=[Zn], writes=[t1n])
        yield
        A("act", lambda e: e.activation(out=t1[:], in_=t1[:], func=AF.Exp, scale=-1.0), reads=[t1n], writes=[t1n])
        yield
        yield
        A("dve", lambda e: e.tensor_tensor(out=t1[:], in0=O_b[z][:, :], in1=t1[:], op=ALU.mult),
          reads=[On, t1n], writes=[t1n])
        slot_free[z] = True
        A("dve", lambda e: e.scalar_tensor_tensor(
            out=dd[:], in0=t1[:, CH:2 * CH], scalar=neglam[:, 0:1], in1=t1[:, 0:CH],
            op0=ALU.mult, op1=ALU.add), reads=[t1n, "neglam"], writes=[ddn])
        yield
        A("pool", lambda e: e.tensor_tensor(out=dsq[:], in0=dd[:], in1=dd[:], op=ALU.mult),
          reads=[ddn], writes=[dsqn])
        yield
        yield
        g = yield from acquire_g()
        A("pe", lambda e: e.matmul(G_b[g][:, 0:CH], ones[:], dsq[:], start=True, stop=True),
          reads=["ones", dsqn], writes=[f"G{g}"])
        yield
        A("act", lambda e: e.activation(out=t3[:], in_=G_b[g][:, 0:CH], func=AF.Ln,
                                        bias=SUBLN_EPS, scale=1.0 / 128.0), reads=[f"G{g}"], writes=[t3n])
        gfree[g] = True
        yield
        A("act", lambda e: e.activation(out=t3[:], in_=t3[:], func=AF.Exp, scale=-0.5), reads=[t3n], writes=[t3n])
        yield
        yield
        A("dve", lambda e: e.tensor_tensor(out=t3[:], in0=dd[:], in1=t3[:], op=ALU.mult),
          reads=[ddn, t3n], writes=[t3n])
        e1p = st["E1"].get(c - 1)
        while e1p is not None and not e1p.done:
            yield
        A("dve", lambda e: e.scalar_tensor_tensor(
            out=mixedT[:, 4 + h, :], in0=t3[:], scalar=sgc[:, 0:1], in1=gs[:, 4 + h, :],
            op0=ALU.mult, op1=ALU.mult), reads=[t3n, "sgc", f"gs{4 + h}"], writes=[f"mx{4 + h}"])

    st = {"epi_s": None, "epi_d": None, "A": {}, "E1": {}, "E2": {}}
    slot_free = [True, True]

    def take_slot():
        z = rot("oz", 2)
        guard = 0
        while not slot_free[z]:
            pump()
            guard += 1
            assert guard < 10000
        slot_free[z] = False
        return z


    class Unit:
        def s1(self): pass
        def s2(self): pass
        def s3(self): pass
        def post(self): pass

    class ProjUnit(Unit):
        def __init__(self, c, fbs, last=False):
            self.c, self.fbs, self.last = c, list(fbs), last

        def s1(self):
            c = self.c
            assert c in st["A"]
            wait_chain(st["A"].get(c))
            self.si = rot("s", NS)
            Sn = f"S{self.si}"
            for k, fb in enumerate(self.fbs):
                pj = S_b[self.si][:, k * CH:(k + 1) * CH]
                wr = wres(fb * 128, (fb + 1) * 128)
                for dc in range(NDC):
                    A("pe", lambda e, dc=dc, pj=pj, fb=fb: e.matmul(
                        pj, wbf[:, dc, fb * 128:(fb + 1) * 128], hT[:, dc, :],
                        start=(dc == 0), stop=(dc == NDC - 1)), reads=wr + ["sqh"], writes=[Sn])

        def s2(self):
            c, fbs, si = self.c, self.fbs, self.si
            if self.last and c + 1 < NCH:
                st["A"][c + 1] = start_chain(chain_A(c + 1), after=[st["E2"].get(c - 1)])
            n = len(fbs)
            Sn = f"S{si}"
            bank = S_b[si]
            b3 = bank[:, 0:n * CH].rearrange("p (a q) -> p a q", a=n)
            tok = slice(c * CH, (c + 1) * CH)
            fb = fbs[0]
            if fb < 4:
                evac(QTa[0:64, fb:fb + n, 0, :], b3[0:64], [Sn], [f"QTa{f}_0" for f in fbs], eng="act")
                evac(QTa[64:128, fb:fb + n, 1, :], b3[64:128], [Sn], [f"QTa{f}_1" for f in fbs])
            elif fb == 4:
                evac(KTa[:, tok], bank[:, 0:CH], [Sn], [f"KTa{c}"])
            elif fb < 9:
                h = fb - 5
                evac(QTb[0:64, h:h + n, 0, :], b3[0:64], [Sn], [f"QTb{f - 5}_0" for f in fbs], eng="act")
                evac(QTb[64:128, h:h + n, 1, :], b3[64:128], [Sn], [f"QTb{f - 5}_1" for f in fbs])
            elif fb < 13:
                h = fb - 9
                evac(KTb[:, h:h + n, tok], b3, [Sn], [f"KTb{f - 9}_{c}" for f in fbs])
            else:
                j = fb - 13
                wait_chain(st["epi_s"] if j < 4 else st["epi_d"])
                gtn, gtt = "gt0", gt[0]
                pjn = bank[:, 0:n * CH]
                A("act", lambda e: e.activation(out=gtt[:, 0:n * CH], in_=pjn, func=AF.Exp, scale=-1.0),
                  reads=[Sn], writes=[gtn])
                A("act", lambda e: e.activation(out=gtt[:, 0:n * CH], in_=gtt[:, 0:n * CH], func=AF.Ln,
                                                bias=1.0, scale=1.0), reads=[gtn], writes=[gtn])
                A("act", lambda e: e.activation(out=gtt[:, 0:n * CH], in_=gtt[:, 0:n * CH], func=AF.Exp,
                                                scale=-1.0), reads=[gtn], writes=[gtn])
                A("dve", lambda e: e.tensor_tensor(out=gs[:, j:j + n, :].rearrange("p a q -> p (a q)"), in0=pjn,
                                                   in1=gtt[:, 0:n * CH], op=ALU.mult),
                  reads=[Sn, gtn], writes=[f"gs{f - 13}" for f in fbs])


    class VUnit(Unit):
        def __init__(self, c, tb, kind):
            self.c, self.tb, self.kind = c, tb, kind

        def s1(self):
            tb = self.tb
            assert self.c in st["A"]
            wait_chain(st["A"].get(self.c))
            self.si = rot("s", NS)
            Sn = f"S{self.si}"
            if self.kind == "b":
                out, lo, hi = S_b[self.si][:, :], VOFF + 128, VOFF + 640
            else:
                out, lo, hi = S_b[self.si][:, 0:128], VOFF, VOFF + 128
            wr = wres(lo, hi)
            for dc in range(NDC):
                A("pe", lambda e, dc=dc: e.matmul(
                    out, hT[:, dc, tb * 128:(tb + 1) * 128], wbf[:, dc, lo:hi],
                    start=(dc == 0), stop=(dc == NDC - 1)), reads=wr + ["sqh"], writes=[Sn])

        def s2(self):
            blk = 2 * self.c + self.tb
            Sn = f"S{self.si}"
            if self.kind == "b":
                for h in range(4):
                    A("dve", lambda e, h=h: e.tensor_scalar(
                        out=Vb[:, blk, h * 128:(h + 1) * 128], in0=S_b[self.si][:, h * 128:(h + 1) * 128],
                        scalar1=gc[:, 17 + h:18 + h], scalar2=None, op0=ALU.mult),
                      reads=[Sn, "gc"], writes=[f"Vb{blk}"])
            else:
                evac(Va[:, blk, :], S_b[self.si][:, 0:128], [Sn], [f"Va{blk}"])


    class SwaUnit(Unit):
        def __init__(self, c, b, il):
            self.c, self.b, self.il = c, b, il

        def s1(self):
            c, b, il = self.c, self.b, self.il
            i = 2 * c + il
            self.slots = [1] if i == 0 else [0, 1]
            sl0 = self.slots[0]
            self.si = rot("s", NS)
            Sn = f"S{self.si}"
            S4 = S_b[self.si][:].rearrange("p (s m q) -> p s m q", s=2, m=2)
            A("pe", lambda e: e.matmul(S4[:, sl0:2, :, :], ident, band(b)[:, sl0:2, :, :], start=True, stop=False),
              reads=["cbf"], writes=[Sn])
            for sl in self.slots:
                kblk = i - 1 + sl
                A("pe", lambda e, sl=sl, kblk=kblk: e.matmul(
                    S4[:, sl, :, :], KTa[:, kblk * 128:(kblk + 1) * 128],
                    QTa[:, b, :, il * 128:(il + 1) * 128], start=False, stop=(sl == 1)),
                  reads=[f"KTa{kblk // 2}", f"QTa{b}_0", f"QTa{b}_1"], writes=[Sn])

        def s2(self):
            sl0 = self.slots[0]
            S4 = S_b[self.si][:].rearrange("p (s m q) -> p s m q", s=2, m=2)
            self.pi = rot("p", 3)
            P4 = Pt_[self.pi][:].rearrange("p (s m q) -> p s m q", s=2, m=2)
            A("act", lambda e: e.activation(out=P4[:, sl0:2, :, :], in_=S4[:, sl0:2, :, :], func=AF.Exp, scale=0.125),
              reads=[f"S{self.si}"], writes=[f"P{self.pi}"])

        def s3(self):
            c, b, il = self.c, self.b, self.il
            i = 2 * c + il
            sl0 = self.slots[0]
            if il == 0:
                st[("swaz", c, b)] = take_slot()
            z = st[("swaz", c, b)]
            On, Zn, Pn = f"O{z}", f"Z{z}", f"P{self.pi}"
            O4 = O_b[z][:].rearrange("p (i m q) -> p i m q", i=2, m=2)
            Z4 = Z_b[z][:].rearrange("p (i m q) -> p i m q", i=2, m=2)
            P4 = Pt_[self.pi][:].rearrange("p (s m q) -> p s m q", s=2, m=2)
            for sl in self.slots:
                kblk = i - 1 + sl
                A("pe", lambda e, sl=sl, kblk=kblk: e.matmul(
                    O4[:, il, :, :], Va[:, kblk, :], P4[:, sl, :, :], start=(sl == sl0), stop=(sl == 1)),
                  reads=[f"Va{kblk}", Pn], writes=[On])
            for sl in self.slots:
                A("pe", lambda e, sl=sl: e.matmul(
                    Z4[:, il, :, :], ones[:], P4[:, sl, :, :], start=(sl == sl0), stop=(sl == 1)),
                  reads=["ones", Pn], writes=[Zn])

        def post(self):
            if self.il == 1:
                z = st[("swaz", self.c, self.b)]
                ch = start_chain(chain_swa_epi(self.c, self.b, z), after=[st["epi_s"]])
                st["epi_s"] = ch

    class DiffUnit(Unit):
        def __init__(self, c, h, kb):
            self.c, self.h, self.kb = c, h, kb
            self.nkb = 2 * c + 2

        def s1(self):
            c, h, kb = self.c, self.h, self.kb
            if h == 0 and kb == 0 and c > 0:
                wait_chain(st["E2"].get(c - 1))
            self.si = rot("s", NS)
            Sn = f"S{self.si}"
            Sb = S_b[self.si]
            diag = kb >= 2 * c
            if diag:
                t = kb - 2 * c
                A("pe", lambda e: e.matmul(Sb[:, :], ident, stair(t), start=True, stop=False),
                  reads=["cbf"], writes=[Sn])
            A("pe", lambda e: e.matmul(
                Sb[:, :].rearrange("p (m q) -> p m q", m=2), KTb[:, h, kb * 128:(kb + 1) * 128],
                QTb[:, h, :, :], start=(not diag), stop=True),
              reads=[f"KTb{h}_{kb // 2}", f"QTb{h}_0", f"QTb{h}_1"], writes=[Sn])

        def s2(self):
            c, h, kb = self.c, self.h, self.kb
            self.pi = rot("p", 3)
            imm = float(2.0 ** (-2.0 * (h + 1)) * (128.0 * (kb - 2 * c) - 128.0))
            Sb = S_b[self.si]
            Pt = Pt_[self.pi]
            A("act", lambda e: e.activation(out=Pt[:], in_=Sb[:, :], func=AF.Exp, bias=imm, scale=0.125),
              reads=[f"S{self.si}"], writes=[f"P{self.pi}"])

        def s3(self):
            c, h, kb, nkb = self.c, self.h, self.kb, self.nkb
            if kb == 0:
                st[("dz", c, h)] = take_slot()
            z = st[("dz", c, h)]
            Pt = Pt_[self.pi]
            Pn = f"P{self.pi}"
            A("pe", lambda e: e.matmul(O_b[z][:, :], Vb[:, kb, h * 128:(h + 1) * 128], Pt[:],
                                       start=(kb == 0), stop=(kb == nkb - 1)),
              reads=[f"Vb{kb}", Pn], writes=[f"O{z}"])
            A("pe", lambda e: e.matmul(Z_b[z][:, :], wtile(h), Pt[:], start=(kb == 0), stop=(kb == nkb - 1)),
              reads=["cbf", Pn], writes=[f"Z{z}"])

        def post(self):
            c, h = self.c, self.h
            if self.kb == self.nkb - 1:
                z = st[("dz", c, h)]
                ch = start_chain(chain_diff_epi(c, h, z), after=[st["epi_d"]], rate=(2 if h == 3 else 1))
                st["epi_d"] = ch
                if h == 3:
                    e1 = start_chain(chain_E1(c), after=[ch, st["epi_s"]], rate=3)
                    st["E1"][c] = e1
                    st["E2"][c] = start_chain(chain_E2(c), after=[e1], rate=3)

    units = []

    def kv_units(c):
        vu = [VUnit(c, 0, "b"), VUnit(c, 0, "a"), VUnit(c, 1, "b"), VUnit(c, 1, "a")]
        return [ProjUnit(c, (4,)), ProjUnit(c, (9, 10)), ProjUnit(c, (11, 12))] + vu

    for c in range(NCH):
        gu = [ProjUnit(c, (13 + 2 * k, 14 + 2 * k)) for k in range(4)]
        gu[3].last = True
        qu = [ProjUnit(c, (0, 1)), ProjUnit(c, (2, 3)), ProjUnit(c, (5, 6)), ProjUnit(c, (7, 8))]
        if c == 0:
            kv = kv_units(0)
            units += qu[0:2] + [kv[0]] + qu[2:4] + kv[1:3] + [kv[3], gu[0], kv[4], kv[5], gu[1], kv[6]]
        else:
            units += qu + [gu[0], gu[1]]
        swa = [[SwaUnit(c, b, il) for il in range(2)] for b in range(4)]
        dif = [[DiffUnit(c, h, kb) for kb in range(2 * c + 2)] for h in range(4)]
        units += swa[0] + [gu[2]] + swa[1] + [gu[3]]
        nxt = kv_units(c + 1) if c + 1 < NCH else []
        fill = [nxt[0:2], nxt[2:4], nxt[4:6], nxt[6:7]] if nxt else [[], [], [], []]
        units += dif[0] + fill[0] + dif[1] + fill[1] + swa[2] + swa[3] + dif[2] + fill[2] + dif[3] + fill[3]

    st["A"][0] = start_chain(chain_A(0))
    n_units = len(units)
    LA = NS - 1
    for k in range(LA):
        units[k].s1()
    for i, u in enumerate(units):
        u.s2()
        if i + LA < n_units:
            units[i + LA].s1()
        if i > 0:
            units[i - 1].s3()
            units[i - 1].post()
        pump()
    units[-1].s3()
    units[-1].post()
    flush()

    last_store = [op for op in tr.ops if op.is_dma and op.group == "st"][-1]
    fin = _Op("sp", None)
    fin.deps = [last_store]
    tr.ops.append(fin)

    seqc = {e: 0 for e in _Tracker.ENGS}
    for op in tr.ops:
        if op.signal and not op.is_dma:
            seqc[op.eng] += 1
            op.seq = seqc[op.eng]
    groups = sorted(tr.group_cnt.keys())
    sem_eng = {e: es.enter_context(nc.semaphore(f"s_{e}")) for e in _Tracker.ENGS}
    sem_dma = {g: es.enter_context(nc.semaphore(f"d_{g}")) for g in groups}

    def emit_engine(eng_name, handle):
        waited = {}
        for op in tr.ops:
            if op.eng != eng_name:
                continue
            need = {}
            for d in op.deps:
                if d.is_dma:
                    key = ("d", d.group)
                    val = d.cum
                else:
                    key = ("e", d.eng)
                    val = d.seq
                if val > need.get(key, 0):
                    need[key] = val
            for key, val in need.items():
                if waited.get(key, 0) >= val:
                    continue
                sem = sem_dma[key[1]] if key[0] == "d" else sem_eng[key[1]]
                handle.wait_ge(sem, val)
                waited[key] = val
            if op.emit is None:
                continue
            ins = op.emit(handle)
            if op.is_dma:
                ins.then_inc(sem_dma[op.group], 16)
            elif op.signal:
                ins.then_inc(sem_eng[op.eng], 1)

    with nc.Block() as block:
        @block.tensor
        def _(e):
            emit_engine("pe", e)

        @block.scalar
        def _(e):
            emit_engine("act", e)

        @block.vector
        def _(e):
            emit_engine("dve", e)

        @block.gpsimd
        def _(e):
            emit_engine("pool", e)

        @block.sync
        def _(e):
            emit_engine("sp", e)

    es.close()
    return nc


_CACHE = {}


def kernel(x, norm_g, w_in, sinks, lambda_q1, lambda_k1, lambda_q2, lambda_k2, subln_g, w_out, final_g):
    x = np.asarray(x, dtype=np.float32)
    B = x.shape[0]
    assert B == NCORES and x.shape[1] == SEQ and x.shape[2] == D_MODEL
    wperm = np.ascontiguousarray(np.asarray(w_in, np.float32)[0][:, _win_perm()])
    woperm = np.ascontiguousarray(np.asarray(w_out, np.float32)[0][_wout_perm(), :])
    gcols = np.zeros((128, 21), np.float32)
    gcols[:, 0:8] = np.asarray(norm_g, np.float32)[0].reshape(8, 128).T
    gcols[:, 8:16] = np.asarray(final_g, np.float32).reshape(8, 128).T
    gcols[:, 16] = np.asarray(subln_g, np.float32)[0]
    lamv = np.concatenate([np.asarray(a, np.float32)[0] for a in
                           (lambda_q1, lambda_k1, lambda_q2, lambda_k2)]).reshape(1, 256)
    sk = np.asarray(sinks, np.float32).reshape(1, 8)
    cbf, wr = _const_tables()
    gcols[:, 17:21] = wr
    if "nc" not in _CACHE:
        _CACHE["nc"] = build_nc()
    nc = _CACHE["nc"]
    in_maps = []
    for b in range(B):
        in_maps.append({
            "xT": np.ascontiguousarray(x[b].T),
            "w_in": wperm, "w_out": woperm, "gcols": gcols, "sinks": sk, "lamv": lamv,
            "cbf": cbf,
        })
    res = run_bass_kernel_spmd(nc, in_maps, core_ids=list(range(NCORES)))
    out = np.empty((B, SEQ, D_MODEL), np.float32)
    for b in range(B):
        out[b] = res.results[b]["outT"].T
    return out
```

```python
import numpy as np
import concourse.bass as bass
import concourse.mybir as mybir
from concourse.bass_utils import run_bass_kernel_spmd

F32 = mybir.dt.float32
BF16 = mybir.dt.bfloat16
AF = mybir.ActivationFunctionType
ALU = mybir.AluOpType

D_MODEL = 1024
SEQ = 4096
NCORES = 8
CH = 256
NCH = SEQ // CH
NDC = D_MODEL // 128
NFB = 21
VOFF = NFB * 128
NCOLS = 3328
NCBF = 3200 + 512
MASKV = -30000.0
LAMBDA_INIT = 0.2
RMS_EPS = 1e-6
SUBLN_EPS = 1e-5


def _win_perm():
    cols = []
    for b in range(4):
        cols += list(range(64 * b, 64 * b + 64)) + list(range(64 * (4 + b), 64 * (4 + b) + 64))
    cols += list(range(512, 640))
    for h in range(4):
        cols += list(range(768 + 128 * h, 768 + 128 * h + 128))
    for h in range(4):
        cols += list(range(1280 + 128 * h, 1280 + 128 * h + 128))
    for b in range(4):
        cols += list(range(2304 + 64 * b, 2304 + 64 * b + 64))
        cols += list(range(2304 + 64 * (4 + b), 2304 + 64 * (4 + b) + 64))
    for h in range(4):
        cols += list(range(2304 + 512 + 128 * h, 2304 + 512 + 128 * h + 128))
    cols += list(range(640, 768))
    cols += list(range(1792, 2304))
    assert len(cols) == NCOLS and len(set(cols)) == NCOLS
    return np.array(cols)


def _wout_perm():
    rows = []
    for b in range(4):
        rows += list(range(64 * b, 64 * b + 64)) + list(range(64 * (4 + b), 64 * (4 + b) + 64))
    rows += list(range(512, 1024))
    return np.array(rows)


def _const_tables():
    k = np.arange(128)[:, None]
    cbf = np.zeros((128, NCBF), np.float32)
    cbf[:, 0:128] = np.eye(128, dtype=np.float32)
    j = np.arange(256)[None, :]
    for t in range(2):
        m = np.where(j >= 128 * t + k, 0.0, MASKV).astype(np.float32)
        cbf[:, 128 + 512 * t: 128 + 512 * t + 256] = m
        cbf[:, 128 + 512 * t + 256: 128 + 512 * t + 512] = m
    q = np.arange(128)[None, :]
    for b in range(4):
        tile = np.zeros((128, 2, 2, 128), np.float32)
        for m in range(2):
            head = b if m == 0 else 4 + b
            slope = 2.0 ** (-(head + 1))
            d0 = q + 128 - k
            tile[:, 0, m, :] = np.where(k > q, -8.0 * slope * d0, MASKV)
            d1 = q - k
            tile[:, 1, m, :] = np.where(k <= q, -8.0 * slope * d1, MASKV)
        cbf[:, 128 + 1024 + 512 * b: 128 + 1024 + 512 * (b + 1)] = tile.reshape(128, 512)
    import ml_dtypes
    wr = np.zeros((128, 4), np.float32)
    for h in range(4):
        slope = 2.0 ** (-2.0 * (h + 1))
        wr[:, h] = np.exp(slope * np.arange(128)).astype(ml_dtypes.bfloat16).astype(np.float32)
        cbf[:, 3200 + 128 * h: 3200 + 128 * (h + 1)] = wr[:, h:h + 1]
    return cbf, wr


class _Op:
    __slots__ = ("eng", "emit", "deps", "is_dma", "group", "cum", "signal", "seq")

    def __init__(self, eng, emit, is_dma=False, group=None):
        self.eng = eng
        self.emit = emit
        self.deps = []
        self.is_dma = is_dma
        self.group = group
        self.cum = 0
        self.signal = False
        self.seq = 0


class _Tracker:
    ENGS = ("pe", "act", "dve", "pool", "sp")

    def __init__(self):
        self.ops = []
        self.last_w = {}
        self.readers = {}
        self.group_cnt = {}

    def add(self, eng, emit, reads=(), writes=(), dma=None):
        op = _Op(eng, emit, is_dma=dma is not None, group=dma)
        if dma is not None:
            self.group_cnt[dma] = self.group_cnt.get(dma, 0) + 16
            op.cum = self.group_cnt[dma]
        deps = set()
        for r in reads:
            w = self.last_w.get(r)
            if w is not None:
                deps.add(w)
        for w_ in writes:
            w = self.last_w.get(w_)
            if w is not None:
                deps.add(w)
            deps |= self.readers.get(w_, set())
        for d in deps:
            if d is op:
                continue
            if (not d.is_dma) and (not op.is_dma) and d.eng == "pe" and eng == "pe":
                continue
            if d.is_dma and op.is_dma and d.group == op.group:
                continue
            op.deps.append(d)
            if not d.is_dma:
                d.signal = True
        for r in reads:
            self.readers.setdefault(r, set()).add(op)
        for w_ in writes:
            self.last_w[w_] = op
            self.readers[w_] = set()
        self.ops.append(op)
        return op


def build_nc():
    nc = bass.Bass("TRN2", target_bir_lowering=False)
    xT = nc.dram_tensor("xT", [D_MODEL, SEQ], F32, kind="ExternalInput").ap()
    w_in = nc.dram_tensor("w_in", [D_MODEL, NCOLS], F32, kind="ExternalInput").ap()
    w_out = nc.dram_tensor("w_out", [D_MODEL, D_MODEL], F32, kind="ExternalInput").ap()
    gcols = nc.dram_tensor("gcols", [128, 21], F32, kind="ExternalInput").ap()
    sinks = nc.dram_tensor("sinks", [1, 8], F32, kind="ExternalInput").ap()
    lamv = nc.dram_tensor("lamv", [1, 256], F32, kind="ExternalInput").ap()
    cbf_d = nc.dram_tensor("cbf", [128, NCBF], F32, kind="ExternalInput").ap()
    outT = nc.dram_tensor("outT", [D_MODEL, SEQ], F32, kind="ExternalOutput").ap()

    xT_v = xT.rearrange("(dc p) t -> p dc t", p=128)
    outT_v = outT.rearrange("(dc p) t -> p dc t", p=128)

    from contextlib import ExitStack
    es = ExitStack()

    def sb(name, shape, dt):
        return es.enter_context(nc.sbuf_tensor(name, shape, dt))

    wbf = sb("wbf", [128, NDC, NCOLS], BF16)
    woutbf = sb("woutbf", [128, 8, D_MODEL], BF16)
    KTa = sb("KTa", [128, SEQ], BF16)
    KTb = sb("KTb", [128, 4, SEQ], BF16)
    Va = sb("Va", [128, 32, 128], BF16)
    Vb = sb("Vb", [128, 32, 512], BF16)
    cbf = sb("cbfs", [128, NCBF], BF16)
    gc = sb("gcs", [128, 21], F32)
    esink = sb("esink", [128, 8], F32)
    cexp = sb("cexp", [128, 2], F32)
    lam2 = sb("lam2", [128, 2], F32)
    neglam = sb("neglam", [128, 1], F32)
    sgc = sb("sgc", [128, 1], F32)
    ones = sb("ones", [128, 128], BF16)
    xs = [sb(f"xs{i}", [128, NDC, CH], F32) for i in range(2)]
    sq = sb("sqh", [128, NDC, CH], BF16)
    hT = sq
    QTa = sb("QTa", [128, 4, 2, CH], BF16)
    QTb = sb("QTb", [128, 4, 2, CH], BF16)
    gs = sb("gs", [128, 8, CH], BF16)
    mixedT = sb("mixedT", [128, 8, CH], BF16)
    Pt_ = [sb(f"Pt{i}", [128, 512], BF16) for i in range(3)]
    rstdA = sb("rstdA", [128, CH], F32)
    rstdT = sb("rstdT", [128, CH], F32)
    gt = [sb("gt0", [128, 2 * CH], F32)]
    T1 = [sb(f"T1_{i}", [128, 512], F32) for i in range(1)]
    oc = sb("oc", [128, 512], F32)
    t1s = sb("t1s", [128, 512], F32)
    sqt = T1[0][:].bitcast(BF16).rearrange("p (a b) -> p a b", a=4)
    lamt = oc[:, 0:256]
    lamp = oc[:, 256:384]
    Dd = [sb(f"Dd{i}", [128, CH], F32) for i in range(1)]
    Dsq = [sb(f"Dsq{i}", [128, CH], BF16) for i in range(1)]
    T3 = [sb(f"T3_{i}", [128, CH], F32) for i in range(1)]

    banks = [es.enter_context(nc.psum_tensor(f"bank{i}", [128, 512], F32)) for i in range(8)]
    NS = 3
    S_b = banks[0:3]
    O_b = banks[3:5]
    Z_b = banks[5:7]
    G_b = banks[7:8]

    ident = cbf[:, 0:128]

    def stair(t):
        return cbf[:, 128 + 512 * t: 128 + 512 * (t + 1)]

    def wtile(h):
        return cbf[:, 3200 + 128 * h: 3200 + 128 * (h + 1)]

    def band(b):
        return cbf[:, 128 + 1024 + 512 * b: 128 + 1024 + 512 * (b + 1)].rearrange(
            "p (s m q) -> p s m q", s=2, m=2)

    tr = _Tracker()
    A = tr.add
    cnt = {"s": 0, "oz": 0, "p": 0, "gt": 0, "sqt": 0}

    def rot(key, n):
        v = cnt[key] % n
        cnt[key] += 1
        return v

    class Chain:
        def __init__(self, gen, after=(), rate=1):
            self.gen = gen
            self.after = [a for a in after if a is not None]
            self.done = False
            self.rate = rate

    chains = []
    gfree = [True]

    def start_chain(gen, after=(), rate=1):
        ch = Chain(gen, after, rate)
        chains.append(ch)
        return ch

    def _step(ch):
        if ch.done:
            return
        if any(not a.done for a in ch.after):
            return
        try:
            next(ch.gen)
        except StopIteration:
            ch.done = True

    def pump():
        for ch in list(chains):
            for _ in range(ch.rate):
                _step(ch)
        chains[:] = [ch for ch in chains if not ch.done]

    def wait_chain(ch):
        guard = 0
        while ch is not None and not ch.done:
            pump()
            guard += 1
            assert guard < 10000

    def flush():
        guard = 0
        while chains:
            pump()
            guard += 1
            assert guard < 10000

    def acquire_g():
        while True:
            for i in range(len(gfree)):
                if gfree[i]:
                    gfree[i] = False
                    return i
            yield

    def load_x(c):
        s = c % 2
        A("sp", lambda e: e.dma_start(out=xs[s][:], in_=xT_v[:, :, c * CH:(c + 1) * CH]),
          writes=[f"xs{s}"], dma=f"ld{s}")

    wgroups = [(0, 640, "wA"), (640, 1664, "wB"), (1664, 2688, "wC"), (2688, 3328, "wD")]

    def wres(lo, hi):
        return [nm for (a, b_, nm) in wgroups if a < hi and lo < b_]

    def load_w(lo, hi, nm):
        for dc in range(NDC):
            A("pool", lambda e, dc=dc: e.dma_start(
                out=wbf[:, dc, lo:hi], in_=w_in[dc * 128:(dc + 1) * 128, lo:hi]),
              writes=[nm], dma=nm)

    load_x(0)
    A("sp", lambda e: e.dma_start(out=gc[:], in_=gcols[:, :]), writes=["gc"], dma="c1")
    A("pool", lambda e: e.memset(ones[:], 1.0), writes=["ones"])
    load_w(*wgroups[0])
    A("pool", lambda e: e.memset(QTa[:].rearrange("p a b c -> p (a b c)"), 0.0),
      writes=[f"QTa{b}_{m}" for b in range(4) for m in range(2)])
    load_w(*wgroups[1])
    A("pool", lambda e: e.memset(QTb[:].rearrange("p a b c -> p (a b c)"), 0.0),
      writes=[f"QTb{b}_{m}" for b in range(4) for m in range(2)])
    load_w(*wgroups[2])
    A("pool", lambda e: e.dma_start(out=cbf[:, 0:1856], in_=cbf_d[:, 0:1856]), writes=["cbf"], dma="c0")
    A("pool", lambda e: e.dma_start(out=cbf[:, 1856:NCBF], in_=cbf_d[:, 1856:NCBF]), writes=["cbf"], dma="c0")
    load_w(*wgroups[3])
    A("sp", lambda e: e.dma_start(out=esink[:], in_=sinks[0:1, :].partition_broadcast(128)),
      writes=["esink"], dma="c2")
    A("sp", lambda e: e.dma_start(out=lamt, in_=lamv[0:1, :].partition_broadcast(128)),
      writes=["oc"], dma="c3")
    A("pool", lambda e: e.memset(cexp[:, 0:1], -1.0), writes=["cexp"])
    A("pool", lambda e: e.memset(cexp[:, 1:2], -0.5), writes=["cexp"])
    load_x(1)
    for j in range(8):
        A("pool", lambda e, j=j: e.dma_start(out=woutbf[:, j, :], in_=w_out[j * 128:(j + 1) * 128, :]),
          writes=["woutbf"], dma="wo")

    A("act", lambda e: e.activation(out=esink[:], in_=esink[:], func=AF.Exp), reads=["esink"], writes=["esink"])
    A("dve", lambda e: e.tensor_tensor(out=lamp.rearrange("p (a b) -> p a b", a=2),
                                       in0=lamt.rearrange("p (a b c) -> p a b c", a=2, b=2)[:, :, 0, :],
                                       in1=lamt.rearrange("p (a b c) -> p a b c", a=2, b=2)[:, :, 1, :],
                                       op=ALU.mult), reads=["oc"], writes=["oc"])
    A("dve", lambda e: e.reduce_sum(out=lam2[:], in_=lamp.rearrange("p (a b) -> p a b", a=2),
                                    axis=mybir.AxisListType.X), reads=["oc"], writes=["lam2"])
    A("act", lambda e: e.activation(out=lam2[:], in_=lam2[:], func=AF.Exp), reads=["lam2"], writes=["lam2"])
    A("dve", lambda e: e.scalar_tensor_tensor(out=neglam[:], in0=lam2[:, 1:2], scalar=-LAMBDA_INIT,
                                              in1=lam2[:, 0:1], op0=ALU.add, op1=ALU.subtract),
      reads=["lam2"], writes=["neglam"])
    A("dve", lambda e: e.tensor_scalar(out=sgc[:], in0=gc[:, 16:17], scalar1=1.0 - LAMBDA_INIT, scalar2=None,
                                       op0=ALU.mult), reads=["gc"], writes=["sgc"])

    def evac(out_ap, in_ap, reads, writes, eng="dve"):
        if eng == "act":
            A("act", lambda e: e.activation(out=out_ap, in_=in_ap, func=AF.Copy), reads=reads, writes=writes)
        else:
            A("dve", lambda e: e.tensor_copy(out=out_ap, in_=in_ap), reads=reads, writes=writes)

    def pow_bc(col, n):
        return cexp[:, col:col + 1].to_broadcast([128, n])

    def chain_A(c):
        s = c % 2
        xsn, xsc = f"xs{s}", xs[s]
        if c > 0:
            for _ in range(10):
                yield
        for half in range(2):
            A("dve", lambda e, half=half: e.tensor_tensor(
                out=sq[:, 4 * half:4 * half + 4, :].rearrange("p a b -> p (a b)"),
                in0=xsc[:, 4 * half:4 * half + 4, :].rearrange("p a b -> p (a b)"),
                in1=xsc[:, 4 * half:4 * half + 4, :].rearrange("p a b -> p (a b)"), op=ALU.mult),
              reads=[xsn], writes=["sqh"])
            yield
        yield
        yield
        g = yield from acquire_g()
        for dc in range(NDC):
            A("pe", lambda e, dc=dc: e.matmul(G_b[g][:, 0:CH], ones[:], sq[:, dc, :],
                                              start=(dc == 0), stop=(dc == NDC - 1)),
              reads=["sqh", "ones"], writes=[f"G{g}"])
        yield
        A("act", lambda e: e.activation(out=rstdA[:], in_=G_b[g][:, 0:CH], func=AF.Ln,
                                        bias=RMS_EPS, scale=1.0 / D_MODEL), reads=[f"G{g}"], writes=["rstdA"])
        gfree[g] = True
        A("act", lambda e: e.activation(out=rstdA[:], in_=rstdA[:], func=AF.Exp, scale=-0.5),
          reads=["rstdA"], writes=["rstdA"])
        yield
        for q4 in range(4):
            for dc in range(2 * q4, 2 * q4 + 2):
                A("dve", lambda e, dc=dc: e.scalar_tensor_tensor(
                    out=hT[:, dc, :], in0=xsc[:, dc, :], scalar=gc[:, dc:dc + 1], in1=rstdA[:],
                    op0=ALU.mult, op1=ALU.mult), reads=[xsn, "gc", "rstdA"], writes=["sqh"])
            yield

    def chain_E1(c):
        s = c % 2
        xsn, xsc = f"xs{s}", xs[s]
        mxall = [f"mx{j}" for j in range(8)]
        for db in range(NDC):
            g = yield from acquire_g()
            for j in range(8):
                A("pe", lambda e, j=j, db=db, g=g: e.matmul(
                    G_b[g][:, 0:CH], woutbf[:, j, db * 128:(db + 1) * 128], mixedT[:, j, :],
                    start=(j == 0), stop=(j == 7)), reads=["woutbf"] + mxall, writes=[f"G{g}"])
            yield
            A("dve", lambda e, db=db, g=g: e.tensor_tensor(
                out=xsc[:, db, :], in0=G_b[g][:, 0:CH], in1=xsc[:, db, :], op=ALU.add),
              reads=[f"G{g}", xsn], writes=[xsn])
            gfree[g] = True
            yield

    def chain_E2(c):
        s = c % 2
        xsn, xsc = f"xs{s}", xs[s]
        g = None
        for half in range(2):
            for dc in range(4 * half, 4 * half + 4):
                A("act", lambda e, dc=dc: e.activation(out=sqt[:, dc % 4, :], in_=xsc[:, dc, :], func=AF.Square),
                  reads=[xsn], writes=["T1_0"])
            yield
            yield
            if g is None:
                g = yield from acquire_g()
            for dc in range(4 * half, 4 * half + 4):
                A("pe", lambda e, dc=dc, g=g: e.matmul(G_b[g][:, 0:CH], ones[:], sqt[:, dc % 4, :],
                                                       start=(dc == 0), stop=(dc == NDC - 1)),
                  reads=["T1_0", "ones"], writes=[f"G{g}"])
            yield
        A("act", lambda e: e.activation(out=rstdT[:], in_=G_b[g][:, 0:CH], func=AF.Ln,
                                        bias=RMS_EPS, scale=1.0 / D_MODEL), reads=[f"G{g}"], writes=["rstdT"])
        gfree[g] = True
        A("act", lambda e: e.activation(out=rstdT[:], in_=rstdT[:], func=AF.Exp, scale=-0.5),
          reads=["rstdT"], writes=["rstdT"])
        yield
        for q4 in range(4):
            for dc in range(2 * q4, 2 * q4 + 2):
                A("dve", lambda e, dc=dc: e.scalar_tensor_tensor(
                    out=xsc[:, dc, :], in0=xsc[:, dc, :], scalar=gc[:, 8 + dc:9 + dc], in1=rstdT[:],
                    op0=ALU.mult, op1=ALU.mult), reads=[xsn, "gc", "rstdT"], writes=[xsn])
            yield
        A("sp", lambda e: e.dma_start(out=outT_v[:, :, c * CH:(c + 1) * CH], in_=xsc[:]),
          reads=[xsn], writes=[f"out{c}"], dma="st")
        if c + 2 < NCH:
            load_x(c + 2)

    def chain_swa_epi(c, b, z):
        On, Zn = f"O{z}", f"Z{z}"
        Z4 = Z_b[z][:].rearrange("p (i m q) -> p i m q", i=2, m=2)
        t1, t1n = t1s, "t1s"
        t14 = t1[:].rearrange("p (i m q) -> p i m q", i=2, m=2)
        for m in range(2):
            head = b if m == 0 else 4 + b
            A("dve", lambda e, m=m, head=head: e.tensor_scalar(
                out=t14[:, :, m, :], in0=Z4[:, :, m, :], scalar1=esink[:, head:head + 1], scalar2=None,
                op0=ALU.add), reads=[Zn, "esink"], writes=[t1n])
        A("dve", lambda e: e.tensor_copy(out=oc[:], in_=O_b[z][:]), reads=[On], writes=["oc"])
        slot_free[z] = True
        yield
        A("act", lambda e: e.activation(out=t1[:], in_=t1[:], func=AF.Ln), reads=[t1n], writes=[t1n])
        yield
        A("act", lambda e: e.activation(out=t1[:], in_=t1[:], func=AF.Exp, scale=-1.0), reads=[t1n], writes=[t1n])
        yield
        yield
        A("dve", lambda e: e.tensor_tensor(out=t1[:], in0=oc[:], in1=t1[:], op=ALU.mult),
          reads=["oc", t1n], writes=[t1n])
        for m in range(2):
            pr = slice(0, 64) if m == 0 else slice(64, 128)
            A("dve", lambda e, m=m, pr=pr: e.tensor_tensor(
                out=mixedT[pr, b, :].rearrange("p (i q) -> p i q", i=2), in0=t14[pr, :, m, :],
                in1=gs[pr, b, :].rearrange("p (i q) -> p i q", i=2), op=ALU.mult),
              reads=[t1n, f"gs{b}"], writes=[f"mx{b}"])

    def chain_diff_epi(c, h, z):
        On, Zn = f"O{z}", f"Z{z}"
        t1, dd, dsq, t3 = T1[0], Dd[0], Dsq[0], T3[0]
        t1n, ddn, dsqn, t3n = "T1_0", "Dd0", "Dsq0", "T3_0"
        A("act", lambda e: e.activation(out=t1[:], in_=Z_b[z][:, :], func=AF.Ln), reads=[Zn], writes=[t1n])
        yield
        A("act", lambda e: e.activation(out=t1[:], in_=t1[:], func=AF.Exp, scale=-1.0), reads=[t1n], writes=[t1n])
        yield
        yield
        A("dve", lambda e: e.tensor_tensor(out=t1[:], in0=O_b[z][:, :], in1=t1[:], op=ALU.mult),
          reads=[On, t1n], writes=[t1n])
        slot_free[z] = True
        A("dve", lambda e: e.scalar_tensor_tensor(
            out=dd[:], in0=t1[:, CH:2 * CH], scalar=neglam[:, 0:1], in1=t1[:, 0:CH],
            op0=ALU.mult, op1=ALU.add), reads=[t1n, "neglam"], writes=[ddn])
        yield
        A("pool", lambda e: e.tensor_tensor(out=dsq[:], in0=dd[:], in1=dd[:], op=ALU.mult),
          reads=[ddn], writes=[dsqn])
        yield
        yield
        g = yield from acquire_g()
        A("pe", lambda e: e.matmul(G_b[g][:, 0:CH], ones[:], dsq[:], start=True, stop=True),
          reads=["ones", dsqn], writes=[f"G{g}"])
        yield
        A("act", lambda e: e.activation(out=t3[:], in_=G_b[g][:, 0:CH], func=AF.Ln,
                                        bias=SUBLN_EPS, scale=1.0 / 128.0), reads=[f"G{g}"], writes=[t3n])
        gfree[g] = True
        yield
        A("act", lambda e: e.activation(out=t3[:], in_=t3[:], func=AF.Exp, scale=-0.5), reads=[t3n], writes=[t3n])
        yield
        yield
        A("dve", lambda e: e.tensor_tensor(out=t3[:], in0=dd[:], in1=t3[:], op=ALU.mult),
          reads=[ddn, t3n], writes=[t3n])
        A("dve", lambda e: e.scalar_tensor_tensor(
            out=mixedT[:, 4 + h, :], in0=t3[:], scalar=sgc[:, 0:1], in1=gs[:, 4 + h, :],
            op0=ALU.mult, op1=ALU.mult), reads=[t3n, "sgc", f"gs{4 + h}"], writes=[f"mx{4 + h}"])

    st = {"epi_s": None, "epi_d": None, "A": {}, "E1": {}, "E2": {}}
    slot_free = [True, True]

    def take_slot():
        z = rot("oz", 2)
        guard = 0
        while not slot_free[z]:
            pump()
            guard += 1
            assert guard < 10000
        slot_free[z] = False
        return z


    class Unit:
        def s1(self): pass
        def s2(self): pass
        def s3(self): pass
        def post(self): pass

    class ProjUnit(Unit):
        def __init__(self, c, fbs, last=False):
            self.c, self.fbs, self.last = c, list(fbs), last

        def s1(self):
            c = self.c
            if self.fbs[0] == 0:
                wait_chain(st["A"].get(c))
            self.si = rot("s", NS)
            Sn = f"S{self.si}"
            for k, fb in enumerate(self.fbs):
                pj = S_b[self.si][:, k * CH:(k + 1) * CH]
                wr = wres(fb * 128, (fb + 1) * 128)
                for dc in range(NDC):
                    A("pe", lambda e, dc=dc, pj=pj, fb=fb: e.matmul(
                        pj, wbf[:, dc, fb * 128:(fb + 1) * 128], hT[:, dc, :],
                        start=(dc == 0), stop=(dc == NDC - 1)), reads=wr + ["sqh"], writes=[Sn])

        def s2(self):
            c, fbs, si = self.c, self.fbs, self.si
            n = len(fbs)
            Sn = f"S{si}"
            bank = S_b[si]
            b3 = bank[:, 0:n * CH].rearrange("p (a q) -> p a q", a=n)
            tok = slice(c * CH, (c + 1) * CH)
            fb = fbs[0]
            if fb < 4:
                evac(QTa[0:64, fb:fb + n, 0, :], b3[0:64], [Sn], [f"QTa{f}_0" for f in fbs], eng="act")
                evac(QTa[64:128, fb:fb + n, 1, :], b3[64:128], [Sn], [f"QTa{f}_1" for f in fbs])
            elif fb == 4:
                evac(KTa[:, tok], bank[:, 0:CH], [Sn], [f"KTa{c}"])
            elif fb < 9:
                h = fb - 5
                evac(QTb[0:64, h:h + n, 0, :], b3[0:64], [Sn], [f"QTb{f - 5}_0" for f in fbs], eng="act")
                evac(QTb[64:128, h:h + n, 1, :], b3[64:128], [Sn], [f"QTb{f - 5}_1" for f in fbs])
            elif fb < 13:
                h = fb - 9
                evac(KTb[:, h:h + n, tok], b3, [Sn], [f"KTb{f - 9}_{c}" for f in fbs])
            else:
                j = fb - 13
                gtn, gtt = "gt0", gt[0]
                pjn = bank[:, 0:n * CH]
                A("act", lambda e: e.activation(out=gtt[:, 0:n * CH], in_=pjn, func=AF.Exp, scale=-1.0),
                  reads=[Sn], writes=[gtn])
                A("act", lambda e: e.activation(out=gtt[:, 0:n * CH], in_=gtt[:, 0:n * CH], func=AF.Ln,
                                                bias=1.0, scale=1.0), reads=[gtn], writes=[gtn])
                A("act", lambda e: e.activation(out=gtt[:, 0:n * CH], in_=gtt[:, 0:n * CH], func=AF.Exp,
                                                scale=-1.0), reads=[gtn], writes=[gtn])
                A("dve", lambda e: e.tensor_tensor(out=gs[:, j:j + n, :].rearrange("p a q -> p (a q)"), in0=pjn,
                                                   in1=gtt[:, 0:n * CH], op=ALU.mult),
                  reads=[Sn, gtn], writes=[f"gs{f - 13}" for f in fbs])

        def post(self):
            c = self.c
            if self.last and c + 1 < NCH:
                st["A"][c + 1] = start_chain(chain_A(c + 1), after=[st["E2"].get(c - 1)])

    class VUnit(Unit):
        def __init__(self, c, tb, kind):
            self.c, self.tb, self.kind = c, tb, kind

        def s1(self):
            tb = self.tb
            self.si = rot("s", NS)
            Sn = f"S{self.si}"
            if self.kind == "b":
                out, lo, hi = S_b[self.si][:, :], VOFF + 128, VOFF + 640
            else:
                out, lo, hi = S_b[self.si][:, 0:128], VOFF, VOFF + 128
            wr = wres(lo, hi)
            for dc in range(NDC):
                A("pe", lambda e, dc=dc: e.matmul(
                    out, hT[:, dc, tb * 128:(tb + 1) * 128], wbf[:, dc, lo:hi],
                    start=(dc == 0), stop=(dc == NDC - 1)), reads=wr + ["sqh"], writes=[Sn])

        def s2(self):
            blk = 2 * self.c + self.tb
            Sn = f"S{self.si}"
            if self.kind == "b":
                for h in range(4):
                    A("dve", lambda e, h=h: e.tensor_scalar(
                        out=Vb[:, blk, h * 128:(h + 1) * 128], in0=S_b[self.si][:, h * 128:(h + 1) * 128],
                        scalar1=gc[:, 17 + h:18 + h], scalar2=None, op0=ALU.mult),
                      reads=[Sn, "gc"], writes=[f"Vb{blk}"])
            else:
                evac(Va[:, blk, :], S_b[self.si][:, 0:128], [Sn], [f"Va{blk}"])


    class SwaUnit(Unit):
        def __init__(self, c, b, il):
            self.c, self.b, self.il = c, b, il

        def s1(self):
            c, b, il = self.c, self.b, self.il
            i = 2 * c + il
            self.slots = [1] if i == 0 else [0, 1]
            sl0 = self.slots[0]
            self.si = rot("s", NS)
            Sn = f"S{self.si}"
            S4 = S_b[self.si][:].rearrange("p (s m q) -> p s m q", s=2, m=2)
            A("pe", lambda e: e.matmul(S4[:, sl0:2, :, :], ident, band(b)[:, sl0:2, :, :], start=True, stop=False),
              reads=["cbf"], writes=[Sn])
            for sl in self.slots:
                kblk = i - 1 + sl
                A("pe", lambda e, sl=sl, kblk=kblk: e.matmul(
                    S4[:, sl, :, :], KTa[:, kblk * 128:(kblk + 1) * 128],
                    QTa[:, b, :, il * 128:(il + 1) * 128], start=False, stop=(sl == 1)),
                  reads=[f"KTa{kblk // 2}", f"QTa{b}_0", f"QTa{b}_1"], writes=[Sn])

        def s2(self):
            sl0 = self.slots[0]
            S4 = S_b[self.si][:].rearrange("p (s m q) -> p s m q", s=2, m=2)
            self.pi = rot("p", 3)
            P4 = Pt_[self.pi][:].rearrange("p (s m q) -> p s m q", s=2, m=2)
            A("act", lambda e: e.activation(out=P4[:, sl0:2, :, :], in_=S4[:, sl0:2, :, :], func=AF.Exp, scale=0.125),
              reads=[f"S{self.si}"], writes=[f"P{self.pi}"])

        def s3(self):
            c, b, il = self.c, self.b, self.il
            i = 2 * c + il
            sl0 = self.slots[0]
            if il == 0:
                st[("swaz", c, b)] = take_slot()
            z = st[("swaz", c, b)]
            On, Zn, Pn = f"O{z}", f"Z{z}", f"P{self.pi}"
            O4 = O_b[z][:].rearrange("p (i m q) -> p i m q", i=2, m=2)
            Z4 = Z_b[z][:].rearrange("p (i m q) -> p i m q", i=2, m=2)
            P4 = Pt_[self.pi][:].rearrange("p (s m q) -> p s m q", s=2, m=2)
            for sl in self.slots:
                kblk = i - 1 + sl
                A("pe", lambda e, sl=sl, kblk=kblk: e.matmul(
                    O4[:, il, :, :], Va[:, kblk, :], P4[:, sl, :, :], start=(sl == sl0), stop=(sl == 1)),
                  reads=[f"Va{kblk}", Pn], writes=[On])
            for sl in self.slots:
                A("pe", lambda e, sl=sl: e.matmul(
                    Z4[:, il, :, :], ones[:], P4[:, sl, :, :], start=(sl == sl0), stop=(sl == 1)),
                  reads=["ones", Pn], writes=[Zn])

        def post(self):
            if self.il == 1:
                z = st[("swaz", self.c, self.b)]
                ch = start_chain(chain_swa_epi(self.c, self.b, z), after=[st["epi_s"]])
                st["epi_s"] = ch

    class DiffUnit(Unit):
        def __init__(self, c, h, kb):
            self.c, self.h, self.kb = c, h, kb
            self.nkb = 2 * c + 2

        def s1(self):
            c, h, kb = self.c, self.h, self.kb
            if h == 0 and kb == 0 and c > 0:
                wait_chain(st["E2"].get(c - 1))
            self.si = rot("s", NS)
            Sn = f"S{self.si}"
            Sb = S_b[self.si]
            diag = kb >= 2 * c
            self.q0 = 128 if kb == 2 * c + 1 else 0
            q0 = self.q0
            S3 = Sb[:, :].rearrange("p (m q) -> p m q", m=2)
            if diag:
                t = kb - 2 * c
                A("pe", lambda e: e.matmul(S3[:, :, q0:CH], ident,
                                           stair(t).rearrange("p (m q) -> p m q", m=2)[:, :, q0:CH],
                                           start=True, stop=False), reads=["cbf"], writes=[Sn])
            A("pe", lambda e: e.matmul(
                S3[:, :, q0:CH], KTb[:, h, kb * 128:(kb + 1) * 128],
                QTb[:, h, :, q0:CH], start=(not diag), stop=True),
              reads=[f"KTb{h}_{kb // 2}", f"QTb{h}_0", f"QTb{h}_1"], writes=[Sn])

        def s2(self):
            c, h, kb = self.c, self.h, self.kb
            self.pi = rot("p", 3)
            imm = float(2.0 ** (-2.0 * (h + 1)) * (128.0 * (kb - 2 * c) - 128.0))
            q0 = self.q0
            S3 = S_b[self.si][:, :].rearrange("p (m q) -> p m q", m=2)
            P3 = Pt_[self.pi][:].rearrange("p (m q) -> p m q", m=2)
            A("act", lambda e: e.activation(out=P3[:, :, q0:CH], in_=S3[:, :, q0:CH], func=AF.Exp, bias=imm,
                                            scale=0.125),
              reads=[f"S{self.si}"], writes=[f"P{self.pi}"])

        def s3(self):
            c, h, kb, nkb = self.c, self.h, self.kb, self.nkb
            if kb == 0:
                st[("dz", c, h)] = take_slot()
            z = st[("dz", c, h)]
            q0 = self.q0
            Pn = f"P{self.pi}"
            P3 = Pt_[self.pi][:].rearrange("p (m q) -> p m q", m=2)
            O3 = O_b[z][:, :].rearrange("p (m q) -> p m q", m=2)
            Z3 = Z_b[z][:, :].rearrange("p (m q) -> p m q", m=2)
            A("pe", lambda e: e.matmul(O3[:, :, q0:CH], Vb[:, kb, h * 128:(h + 1) * 128], P3[:, :, q0:CH],
                                       start=(kb == 0), stop=(kb == nkb - 1)),
              reads=[f"Vb{kb}", Pn], writes=[f"O{z}"])
            A("pe", lambda e: e.matmul(Z3[:, :, q0:CH], wtile(h), P3[:, :, q0:CH], start=(kb == 0),
                                       stop=(kb == nkb - 1)),
              reads=["cbf", Pn], writes=[f"Z{z}"])

        def post(self):
            c, h = self.c, self.h
            if self.kb == self.nkb - 1:
                z = st[("dz", c, h)]
                ch = start_chain(chain_diff_epi(c, h, z), after=[st["epi_d"]], rate=(2 if h == 3 else 1))
                st["epi_d"] = ch
                if h == 3:
                    e1 = start_chain(chain_E1(c), after=[ch, st["epi_s"]], rate=3)
                    st["E1"][c] = e1
                    st["E2"][c] = start_chain(chain_E2(c), after=[e1], rate=3)

    units = []
    for c in range(NCH):
        vu = [VUnit(c, 0, "b"), VUnit(c, 0, "a"), VUnit(c, 1, "b"), VUnit(c, 1, "a")]
        gu = [ProjUnit(c, (13 + 2 * k, 14 + 2 * k)) for k in range(4)]
        gu[3].last = True
        units += [ProjUnit(c, (0, 1)), ProjUnit(c, (2, 3)), ProjUnit(c, (4,)), ProjUnit(c, (5, 6)),
                  ProjUnit(c, (7, 8)), ProjUnit(c, (9, 10)), ProjUnit(c, (11, 12)),
                  vu[0], gu[0], vu[1], vu[2], gu[1], vu[3]]
        swa = [[SwaUnit(c, b, il) for il in range(2)] for b in range(4)]
        dif = [[DiffUnit(c, h, kb) for kb in range(2 * c + 2)] for h in range(4)]
        units += swa[0] + [gu[2]] + swa[1] + [gu[3]]
        for grp in (dif[0], dif[1], swa[2], swa[3], dif[2], dif[3]):
            units += grp

    st["A"][0] = start_chain(chain_A(0))
    n_units = len(units)
    LA = NS - 1
    for k in range(LA):
        units[k].s1()
    for i, u in enumerate(units):
        u.s2()
        if i + LA < n_units:
            units[i + LA].s1()
        if i > 0:
            units[i - 1].s3()
            units[i - 1].post()
        pump()
    units[-1].s3()
    units[-1].post()
    flush()

    last_store = [op for op in tr.ops if op.is_dma and op.group == "st"][-1]
    fin = _Op("sp", None)
    fin.deps = [last_store]
    tr.ops.append(fin)

    seqc = {e: 0 for e in _Tracker.ENGS}
    for op in tr.ops:
        if op.signal and not op.is_dma:
            seqc[op.eng] += 1
            op.seq = seqc[op.eng]
    groups = sorted(tr.group_cnt.keys())
    sem_eng = {e: es.enter_context(nc.semaphore(f"s_{e}")) for e in _Tracker.ENGS}
    sem_dma = {g: es.enter_context(nc.semaphore(f"d_{g}")) for g in groups}

    def emit_engine(eng_name, handle):
        waited = {}
        for op in tr.ops:
            if op.eng != eng_name:
                continue
            need = {}
            for d in op.deps:
                if d.is_dma:
                    key = ("d", d.group)
                    val = d.cum
                else:
                    key = ("e", d.eng)
                    val = d.seq
                if val > need.get(key, 0):
                    need[key] = val
            for key, val in need.items():
                if waited.get(key, 0) >= val:
                    continue
                sem = sem_dma[key[1]] if key[0] == "d" else sem_eng[key[1]]
                handle.wait_ge(sem, val)
                waited[key] = val
            if op.emit is None:
                continue
            ins = op.emit(handle)
            if op.is_dma:
                ins.then_inc(sem_dma[op.group], 16)
            elif op.signal:
                ins.then_inc(sem_eng[op.eng], 1)

    with nc.Block() as block:
        @block.tensor
        def _(e):
            emit_engine("pe", e)

        @block.scalar
        def _(e):
            emit_engine("act", e)

        @block.vector
        def _(e):
            emit_engine("dve", e)

        @block.gpsimd
        def _(e):
            emit_engine("pool", e)

        @block.sync
        def _(e):
            emit_engine("sp", e)

    es.close()
    return nc


_CACHE = {}


def kernel(x, norm_g, w_in, sinks, lambda_q1, lambda_k1, lambda_q2, lambda_k2, subln_g, w_out, final_g):
    x = np.asarray(x, dtype=np.float32)
    B = x.shape[0]
    assert B == NCORES and x.shape[1] == SEQ and x.shape[2] == D_MODEL
    wperm = np.ascontiguousarray(np.asarray(w_in, np.float32)[0][:, _win_perm()])
    woperm = np.ascontiguousarray(np.asarray(w_out, np.float32)[0][_wout_perm(), :])
    gcols = np.zeros((128, 21), np.float32)
    gcols[:, 0:8] = np.asarray(norm_g, np.float32)[0].reshape(8, 128).T
    gcols[:, 8:16] = np.asarray(final_g, np.float32).reshape(8, 128).T
    gcols[:, 16] = np.asarray(subln_g, np.float32)[0]
    lamv = np.concatenate([np.asarray(a, np.float32)[0] for a in
                           (lambda_q1, lambda_k1, lambda_q2, lambda_k2)]).reshape(1, 256)
    sk = np.asarray(sinks, np.float32).reshape(1, 8)
    cbf, wr = _const_tables()
    gcols[:, 17:21] = wr
    if "nc" not in _CACHE:
        _CACHE["nc"] = build_nc()
    nc = _CACHE["nc"]
    in_maps = []
    for b in range(B):
        in_maps.append({
            "xT": np.ascontiguousarray(x[b].T),
            "w_in": wperm, "w_out": woperm, "gcols": gcols, "sinks": sk, "lamv": lamv,
            "cbf": cbf,
        })
    res = run_bass_kernel_spmd(nc, in_maps, core_ids=list(range(NCORES)))
    out = np.empty((B, SEQ, D_MODEL), np.float32)
    for b in range(B):
        out[b] = res.results[b]["outT"].T
    return out
```
